# Optimizing a Trainium2 kernel written in Bass

```python
import jax, jax.numpy as jnp
from jax import lax
import numpy as np

D_MODEL = 2048
BATCH = 1
SEQ = 16384
DEPTH = 1
DEC_BATCH = 16
DEC_SEQ = 64
PAST_LEN = 4096

CHUNK = 64
D_A = D_MODEL // 2
D_B = D_MODEL - D_A
CONV_A_W = 3
CONV_B_W = 31
D_IN = 3 * D_A + 2 * D_B
D_FF = 5632
EPS = 1e-6

kernel_name = "hybrid_shortconv_conformer_stream_step"


def _rmsnorm(x, g):
    x32 = x.astype(jnp.float32)
    y = x32 * lax.rsqrt(jnp.mean(x32 * x32, axis=-1, keepdims=True) + EPS)
    return (y * g.astype(jnp.float32)).astype(x.dtype)


def _layernorm(x, g, b):
    x32 = x.astype(jnp.float32)
    mu = jnp.mean(x32, axis=-1, keepdims=True)
    xc = x32 - mu
    var = jnp.mean(xc * xc, axis=-1, keepdims=True)
    y = xc * lax.rsqrt(var + EPS)
    return (y * g.astype(jnp.float32) + b.astype(jnp.float32)).astype(x.dtype)


def _swiglu(h, w_gate, w_up, w_down):
    return (jax.nn.silu(h @ w_gate) * (h @ w_up)) @ w_down


def _causal_dwconv(u, past, w):
    width = w.shape[0]
    ext = jnp.concatenate([past.astype(u.dtype), u], axis=1)
    out = lax.conv_general_dilated(
        ext, w[:, None, :].astype(u.dtype), window_strides=(1,), padding="VALID",
        dimension_numbers=("NWC", "WIO", "NWC"), feature_group_count=u.shape[-1])
    return out, ext[:, ext.shape[1] - (width - 1):, :]


def _layer(x, st_a, st_b, ffn1_norm, ffn1_w_gate, ffn1_w_up, ffn1_w_down, mix_norm, w_in,
           conv_a_w, conv_b_w, conv_b_bias, conv_b_ln_g, conv_b_ln_b, w_out,
           ffn2_norm, ffn2_w_gate, ffn2_w_up, ffn2_w_down):
    x = x + 0.5 * _swiglu(_rmsnorm(x, ffn1_norm), ffn1_w_gate, ffn1_w_up, ffn1_w_down)
    h = _rmsnorm(x, mix_norm)
    z = h @ w_in
    b_gate, c_gate, v_a, u_b, g_b = jnp.split(
        z, [D_A, 2 * D_A, 3 * D_A, 3 * D_A + D_B], axis=-1)
    conv_a, new_a = _causal_dwconv(c_gate * v_a, st_a, conv_a_w)
    y_a = b_gate * conv_a
    u = u_b * jax.nn.sigmoid(g_b)
    conv_b, new_b = _causal_dwconv(u, st_b, conv_b_w)
    y_b = jax.nn.silu(_layernorm(conv_b + conv_b_bias.astype(conv_b.dtype), conv_b_ln_g, conv_b_ln_b))
    x = x + jnp.concatenate([y_a, y_b], axis=-1) @ w_out
    x = x + 0.5 * _swiglu(_rmsnorm(x, ffn2_norm), ffn2_w_gate, ffn2_w_up, ffn2_w_down)
    return x, new_a, new_b


def setup_inputs(seed: int = 0) -> dict:
    key = jax.random.key(seed)
    ks = jax.random.split(key, 20)
    f32 = jnp.float32
    nrm = lambda k, shape, scale: jax.random.normal(k, shape, f32) * scale
    gain = lambda k, shape: 1.0 + 0.01 * jax.random.normal(k, shape, f32)
    return {
        "x_prompt": nrm(ks[0], (BATCH, SEQ, D_MODEL), 1.0),
        "x_sample": nrm(ks[1], (DEC_BATCH, DEC_SEQ, D_MODEL), 1.0),
        "state_conv_a": nrm(ks[2], (DEPTH, DEC_BATCH, CONV_A_W - 1, D_A), 1.0),
        "state_conv_b": nrm(ks[3], (DEPTH, DEC_BATCH, CONV_B_W - 1, D_B), 1.0),
        "ffn1_norm": gain(ks[4], (DEPTH, D_MODEL)),
        "ffn1_w_gate": nrm(ks[5], (DEPTH, D_MODEL, D_FF), D_MODEL ** -0.5),
        "ffn1_w_up": nrm(ks[6], (DEPTH, D_MODEL, D_FF), D_MODEL ** -0.5),
        "ffn1_w_down": nrm(ks[7], (DEPTH, D_FF, D_MODEL), D_FF ** -0.5),
        "mix_norm": gain(ks[8], (DEPTH, D_MODEL)),
        "w_in": nrm(ks[9], (DEPTH, D_MODEL, D_IN), D_MODEL ** -0.5),
        "conv_a_w": nrm(ks[10], (DEPTH, CONV_A_W, D_A), CONV_A_W ** -0.5),
        "conv_b_w": nrm(ks[11], (DEPTH, CONV_B_W, D_B), CONV_B_W ** -0.5),
        "conv_b_bias": nrm(ks[12], (DEPTH, D_B), 0.01),
        "conv_b_ln_g": gain(ks[13], (DEPTH, D_B)),
        "conv_b_ln_b": nrm(ks[14], (DEPTH, D_B), 0.01),
        "w_out": nrm(ks[15], (DEPTH, D_A + D_B, D_MODEL), (D_A + D_B) ** -0.5),
        "ffn2_norm": gain(ks[16], (DEPTH, D_MODEL)),
        "ffn2_w_gate": nrm(ks[17], (DEPTH, D_MODEL, D_FF), D_MODEL ** -0.5),
        "ffn2_w_up": nrm(ks[18], (DEPTH, D_MODEL, D_FF), D_MODEL ** -0.5),
        "ffn2_w_down": nrm(ks[19], (DEPTH, D_FF, D_MODEL), D_FF ** -0.5),
        "final_norm": gain(jax.random.fold_in(key, 99), (D_MODEL,)),
    }


def reference(x_prompt, x_sample, state_conv_a, state_conv_b, ffn1_norm, ffn1_w_gate,
              ffn1_w_up, ffn1_w_down, mix_norm, w_in, conv_a_w, conv_b_w, conv_b_bias,
              conv_b_ln_g, conv_b_ln_b, w_out, ffn2_norm, ffn2_w_gate, ffn2_w_up,
              ffn2_w_down, final_norm):
    bp = x_prompt.shape[0]
    xp, xs = x_prompt, x_sample
    pa_list, pb_list, sa_list, sb_list = [], [], [], []
    for l in range(DEPTH):
        w = (ffn1_norm[l], ffn1_w_gate[l], ffn1_w_up[l], ffn1_w_down[l], mix_norm[l], w_in[l],
             conv_a_w[l], conv_b_w[l], conv_b_bias[l], conv_b_ln_g[l], conv_b_ln_b[l], w_out[l],
             ffn2_norm[l], ffn2_w_gate[l], ffn2_w_up[l], ffn2_w_down[l])
        zero_a = jnp.zeros((bp, CONV_A_W - 1, D_A), xp.dtype)
        zero_b = jnp.zeros((bp, CONV_B_W - 1, D_B), xp.dtype)
        xp, pa, pb = _layer(xp, zero_a, zero_b, *w)
        xs, sa, sb = _layer(xs, state_conv_a[l], state_conv_b[l], *w)
        pa_list.append(pa)
        pb_list.append(pb)
        sa_list.append(sa)
        sb_list.append(sb)
    y_prompt = _rmsnorm(xp, final_norm)
    y_sample = _rmsnorm(xs, final_norm)
    new_conv_a_prompt = jnp.stack(pa_list, axis=0)
    new_conv_b_prompt = jnp.stack(pb_list, axis=0)
    new_conv_a_sample = jnp.stack(sa_list, axis=0)
    new_conv_b_sample = jnp.stack(sb_list, axis=0)
    return (y_prompt, y_sample, new_conv_a_prompt, new_conv_b_prompt, new_conv_a_sample, new_conv_b_sample)
```

```python
import contextlib
import collections
import numpy as np
import concourse.bass as bass
import concourse.mybir as mybir
from concourse.bass_utils import run_bass_kernel_spmd

F32 = mybir.dt.float32
BF16 = mybir.dt.bfloat16
AF = mybir.ActivationFunctionType
ALU = mybir.AluOpType
P = 128
NCORES = 8
NT = 6
NPASS = 3
T = NT * P
SA = 2
SB = 30
WA = 3
WB = 31
EPS = 1e-6
HALO = 32
NW1 = 5
NW2 = 2
KDRAIN = 16


class Prog:
    def __init__(self):
        self.ops = {k: [] for k in ("pe", "act", "dve", "pool", "sp")}
        self.cnt = {}
        self.waited = {k: {} for k in self.ops}
        self.semkeys = []

    def _sem(self, key):
        if key not in self.cnt:
            self.cnt[key] = 0
            self.semkeys.append(key)

    def op(self, eng, fn, deps=(), inc=True):
        self._sem(eng)
        waits = self._waits(eng, deps)
        if inc:
            self.cnt[eng] += 1
        self.ops[eng].append((fn, waits, (eng, 1) if inc else None))
        return (eng, self.cnt[eng])

    def dma(self, eng, fn, semkey, deps=()):
        self._sem(semkey)
        waits = self._waits(eng, deps)
        self.cnt[semkey] += 16
        self.ops[eng].append((fn, waits, (semkey, 16)))
        return (semkey, self.cnt[semkey])

    def wait(self, eng, deps):
        waits = self._waits(eng, deps)
        if waits:
            self.ops[eng].append((None, waits, None))

    def _waits(self, eng, deps):
        out = []
        w = self.waited[eng]
        for d in deps:
            if d is None:
                continue
            k, v = d
            if v <= 0:
                continue
            if eng == "pe" and k == "pe":
                continue
            if w.get(k, 0) >= v:
                continue
            w[k] = v
            out.append((k, v))
        return out

    def last(self, eng):
        return (eng, self.cnt.get(eng, 0))


class Ring:
    def __init__(self, prog, name, nslots, loads):
        self.pg, self.name, self.ns, self.loads = prog, name, nslots, loads
        self.ready, self.rel = [], {}
        self.n_issued = 0
        self.n_acq = 0

    def _issue(self, extra=()):
        i = self.n_issued
        if i >= len(self.loads):
            return
        s = i % self.ns
        deps = tuple(self.rel.get(i - self.ns, ()) if i >= self.ns else ()) + tuple(extra)
        mk = self.loads[i]
        tok = self.pg.dma("pool", (lambda e, mk=mk, s=s: mk(e, s)), f"{self.name}{s}", deps)
        self.ready.append(tok)
        self.n_issued += 1

    def prefill(self, deps=()):
        for _ in range(self.ns):
            self._issue(deps)

    def acquire(self):
        i = self.n_acq
        self.n_acq += 1
        assert i < len(self.ready), (self.name, i)
        return i, i % self.ns, self.ready[i]

    def release(self, i, toks):
        self.rel[i] = tuple(toks)
        self.to_issue = getattr(self, "to_issue", 0) + 1
        if not getattr(self, "hold", False):
            self.unhold()

    def unhold(self):
        self.hold = False
        while getattr(self, "to_issue", 0) > 0:
            self.to_issue -= 1
            self._issue()


def build_program(D, FF, G):
    KC = D // P
    FC = FF // P
    NG = FC // G
    DA = D // 2
    CA = DA // P
    CB = CA
    NB = D // 512
    NWIN = 2 * CB + 3 * CA
    NCW = WA + WB + 3
    assert CA <= G

    nc = bass.Bass("TRN2", target_bir_lowering=False)
    dt = nc.dram_tensor
    xin = dt("xin", [NPASS * NT, P, D], F32, kind="ExternalInput").ap()
    wgu = [dt(f"wgu{f}", [FC, 2, P, KC * P], F32, kind="ExternalInput").ap() for f in (1, 2)]
    wdn = [dt(f"wdn{f}", [NG * NB, P, G * 512], F32, kind="ExternalInput").ap() for f in (1, 2)]
    win = dt("win", [NWIN, P, KC * P], F32, kind="ExternalInput").ap()
    wout = dt("wout", [NB * 2, P, CA * 512], F32, kind="ExternalInput").ap()
    nw_d = dt("nw", [P, 3 * KC], F32, kind="ExternalInput").ap()
    cw_d = dt("cw", [P, CB * NCW], F32, kind="ExternalInput").ap()
    gf_d = dt("gfin", [P, D], F32, kind="ExternalInput").ap()
    id_d = dt("ident", [P, P], F32, kind="ExternalInput").ap()
    sa_d = dt("sa_in", [2, SA, DA], F32, kind="ExternalInput").ap()
    sb_d = dt("sb_in", [2, SB, DA], F32, kind="ExternalInput").ap()
    y_d = dt("y", [NPASS * NT - 1, P, D], F32, kind="ExternalOutput").ap()
    oa_d = dt("oa", [3 * SA, DA], F32, kind="ExternalOutput").ap()
    ob_d = dt("ob", [3 * SB, DA], F32, kind="ExternalOutput").ap()

    EXB = 2 * (SB + 64) + SB + (T - 128) + 6
    EXA = 2 * (SA + 64) + SA + (T - 128) + 2
    assert EXB >= SB + T and EXA >= SA + T
    CVB = max(CB * T, 4096)
    JOFF = 4 * DA
    assert JOFF + D // 2 <= CVB

    es = contextlib.ExitStack()
    sb = lambda name, shape, d: es.enter_context(nc.sbuf_tensor(name, shape, d))
    ps = lambda name, shape, d: es.enter_context(nc.psum_tensor(name, shape, d))
    xres = sb("xres", [P, NT, D], F32)
    hT = sb("hT", [P, KC, T], BF16)
    actb = sb("actb", [P, G, T], BF16)
    hb = sb("hb", [P, NT, D], BF16)
    w1 = sb("w1", [P, NW1, KC * P], BF16)
    w2 = sb("w2", [P, NW2, G * 512], BF16)
    convb = sb("convb", [P, CVB], F32)
    extb = sb("extb", [P, 2, EXB], F32)
    exta = sb("exta", [P, 2, EXA], F32)
    tmp = sb("tmp", [P, 4, 384], F32)
    nw = sb("nw_s", [P, 3 * KC], F32)
    cw = sb("cw_s", [P, CB, NCW], F32)
    identf = sb("identf", [P, P], F32)
    identb = sb("identb", [P, P], BF16)
    onesf = sb("onesf", [P, P], F32)
    ss = sb("ss", [P, NT], F32)
    rstd = sb("rstd", [P, NT], F32)
    sd = sb("sd", [P, NT], F32)
    mhalf = sb("mhalf", [P, 1], F32)
    stateb = sb("stateb", [P, CB, SB], F32)
    statea = sb("statea", [P, CA, SA], F32)
    sinb = sb("sinb", [P, CB, 2, SB], F32)
    sina = sb("sina", [P, CA, 2, SA], F32)
    ostb = sb("ostb", [P, CB, 3 * SB], F32)
    osta = sb("osta", [P, CA, 3 * SA], F32)
    pA = [ps(f"pA{i}", [P, 512], F32) for i in range(2)]
    pB = [ps(f"pB{i}", [P, 512], F32) for i in range(2)]
    pD = [ps(f"pD{i}", [P, 512], F32) for i in range(2)]
    pT = [ps(f"pT{i}", [P, 1024], BF16) for i in range(2)]

    tbanks = [pT[0], pT[1], pA[0], pA[1], pB[0], pB[1]]
    tviews = [pT[0][:], pT[1][:]] + [bk[:].bitcast(BF16) for bk in (pA[0], pA[1], pB[0], pB[1])]
    gfin = convb[:, 0:D]
    junk = convb[:, JOFF:JOFF + D // 2].bitcast(BF16)
    pg = Prog()
    op, dma = pg.op, pg.dma

    w1_loads, w2_loads = [], []

    def mk_w1(src):
        return lambda e, s: e.dma_start(out=w1[:, s, :], in_=src)

    def mk_w2(src, n):
        return lambda e, s: e.dma_start(out=w2[:, s, 0:n], in_=src)

    for p_ in range(NPASS):
        for j in range(FC):
            w1_loads += [mk_w1(wgu[0][j, 0]), mk_w1(wgu[0][j, 1])]
        for g in range(NG):
            for nb in range(NB):
                w2_loads.append(mk_w2(wdn[0][g * NB + nb], G * 512))
        for c in range(NWIN):
            w1_loads.append(mk_w1(win[c]))
        for kh in range(2):
            for nb in range(NB):
                w2_loads.append(mk_w2(wout[nb * 2 + kh], CA * 512))
        for j in range(FC):
            w1_loads += [mk_w1(wgu[1][j, 0]), mk_w1(wgu[1][j, 1])]
        for g in range(NG):
            for nb in range(NB):
                w2_loads.append(mk_w2(wdn[1][g * NB + nb], G * 512))
    R1 = Ring(pg, "w1r", NW1, w1_loads)
    R2 = Ring(pg, "w2r", NW2, w2_loads)

    c0tok = None
    for dst, src in ((nw[:], nw_d), (identf[:], id_d)):
        c0tok = dma("sp", (lambda e, dst=dst, src=src: e.dma_start(out=dst, in_=src)), "c0")
    t_mh = op("pool", lambda e: e.memset(mhalf[:], -0.5))
    xload = {}
    for m in range(3):
        xload[m] = dma("sp", (lambda e, m=m: e.dma_start(out=xres[:, m, :], in_=xin[m])), f"x{m}")
    R1._issue()
    R1._issue()
    for m in range(3, NT):
        xload[m] = dma("sp", (lambda e, m=m: e.dma_start(out=xres[:, m, :], in_=xin[m])), f"x{m}",
                       [R1.ready[0], R1.ready[1]])
    stb_stage = convb[0:SB, 0:2 * DA].rearrange("p (s d) -> p s d", s=2)
    sta_stage = convb[0:SA, 2 * DA:4 * DA].rearrange("p (s d) -> p s d", s=2)
    cst = dma("sp", lambda e: e.dma_start(out=cw[:].rearrange("p c k -> p (c k)"), in_=cw_d), "cst")
    for _ in range(NW1 - 2):
        R1._issue([xload[2]])
    t_idb = op("dve", lambda e: e.tensor_copy(out=identb[:], in_=identf[:]), [c0tok])
    t_ones = op("dve", lambda e: e.memset(onesf[:], 1.0))
    op("dve", lambda e: e.memset(extb[:], 0.0))
    op("dve", lambda e: e.memset(exta[:], 0.0))
    op("dve", lambda e: e.memset(actb[:], 0.0))

    def ntiles(c0):
        h = (T - c0) // 2
        return [(c0, h), (c0 + h, T - c0 - h)]

    st = {"pt": 0, "ptfree": {}, "bankfree": {}, "tmpfree": {}, "tmpi": 0, "pdi": 0}

    def bank_dep(b):
        return st["bankfree"].get(id(b), ())

    def set_bank_free(b, toks):
        st["bankfree"][id(b)] = tuple(toks)

    def load_sample_states():
        free = [st.get("convb_free")]
        dma("sp", lambda e: e.dma_start(out=stb_stage, in_=sb_d.rearrange("s r d -> r s d")), "sst", free)
        ld = dma("sp", lambda e: e.dma_start(out=sta_stage, in_=sa_d.rearrange("s r d -> r s d")), "sst", free)
        k = 0
        for (stage, dstt, R) in ((stb_stage, sinb, SB), (sta_stage, sina, SA)):
            for s in range(2):
                for c in range(CB):
                    bank = pD[k % 2]
                    k += 1
                    tp = op("pe", lambda e, bank=bank, stage=stage, s=s, c=c, R=R: e.transpose(
                        out=bank[:, 0:R], in_=stage[:, s, c * P:(c + 1) * P], identity=identf[0:R, 0:R]),
                        [ld, c0tok] + list(bank_dep(bank)))
                    cp = op("dve", lambda e, bank=bank, dstt=dstt, s=s, c=c, R=R: e.tensor_copy(
                        out=dstt[:, c, s, :], in_=bank[:, 0:R]), [tp])
                    set_bank_free(bank, [cp])
        st["sin_done"] = pg.last("dve")

    def rms_group(grp, xready):
        lo, hi = grp[0], grp[-1] + 1
        t1 = None
        for m in grp:
            t1 = op("act", lambda e, m=m: e.activation(out=junk, in_=xres[:, m, :], func=AF.Square,
                                                       scale=float(D) ** -0.5, accum_out=ss[:, m:m + 1]),
                    list(xready(m)) + [st.get("junk_prev"), st.get(("ss_free", m))])
            st["junk_prev"] = t1
        t2 = op("pool", lambda e: e.tensor_scalar(out=sd[:, lo:hi], in0=ss[:, lo:hi], scalar1=EPS,
                                                  scalar2=1.0, op0=ALU.add, op1=ALU.mult),
                [t1] + [st.get(("sd_free", m)) for m in grp])
        t3 = op("pool", lambda e: e.tensor_tensor(out=rstd[:, lo:hi], in0=sd[:, lo:hi],
                                                  in1=mhalf[:, 0:1].to_broadcast([P, hi - lo]), op=ALU.pow),
                [t2, t_mh] + [st.get(("rstd_free", m)) for m in grp])
        for m in grp:
            st[("ss_free", m)] = t2
            st[("sd_free", m)] = t3
        return t3

    def norm_stage(tiles, which, xready):
        groups = [tiles[i:i + 3] for i in range(0, len(tiles), 3)]
        res = []
        rdy = {}
        for grp in groups:
            tk = rms_group(grp, xready)
            for m in grp:
                rdy[m] = tk
        for grp in groups:
            hts = []
            for m in grp:
                t3 = rdy[m]
                t4 = op("dve", lambda e, m=m: e.tensor_scalar(out=hb[:, m, :], in0=xres[:, m, :],
                                                              scalar1=rstd[:, m:m + 1], scalar2=None, op0=ALU.mult),
                        [t3, st.get(("hb_free", m))])
                st[("rstd_free", m)] = t4
                hts.append(t4)
            col0 = grp[0] * P
            w = len(grp) * P
            lastpe = None
            evs = {}
            for kc in range(KC):
                b = st["pt"] % len(tbanks)
                st["pt"] += 1
                bank = tbanks[b]
                buf = tviews[b][:, 0:w]
                for i, m in enumerate(grp):
                    lastpe = op("pe", lambda e, buf=buf, i=i, m=m, kc=kc: e.transpose(
                        out=buf[:, i * P:(i + 1) * P], in_=hb[:, m, kc * P:(kc + 1) * P], identity=identb[:]),
                        hts + list(bank_dep(bank)) + [t_idb], inc=(i == len(grp) - 1))
                sc = nw[:, which * KC + kc:which * KC + kc + 1]
                if kc % 2 == 0:
                    ev = op("dve", lambda e, buf=buf, kc=kc, sc=sc, col0=col0, w=w: e.tensor_scalar(
                        out=hT[:, kc, col0:col0 + w], in0=buf, scalar1=sc, scalar2=None, op0=ALU.mult),
                        [lastpe, st.get("hT_free"), c0tok])
                    evs["dve"] = ev
                else:
                    ev = op("act", lambda e, buf=buf, kc=kc, sc=sc, col0=col0, w=w: e.activation(
                        out=hT[:, kc, col0:col0 + w], in_=buf, func=AF.Identity, scale=sc),
                        [lastpe, st.get("hT_free"), c0tok])
                    evs["act"] = ev
                set_bank_free(bank, [ev])
            for m in grp:
                st[("hb_free", m)] = lastpe
            res.append((col0, col0 + w, list(evs.values())))
        return res

    def hdeps(hgroups, n0, nl):
        out = []
        for (lo, hi, toks) in hgroups:
            if lo < n0 + nl and hi > n0:
                out += toks
        return out

    def ffn(f, c0, tiles, hready):
        nts = ntiles(c0)
        xdone = {}
        for g in range(NG):
            evs = []
            for jj in range(G):
                ig, sg_, rg = R1.acquire()
                iu, su_, ru = R1.acquire()
                lastmm = None
                for n, (n0, nl) in enumerate(nts):
                    tg = tu = None
                    for (bank, slot, rdy) in ((pA[n], sg_, rg), (pB[n], su_, ru)):
                        for kc in range(KC):
                            tk = op("pe", lambda e, bank=bank, slot=slot, kc=kc, n0=n0, nl=nl: e.matmul(
                                bank[:, 0:nl], lhsT=w1[:, slot, kc * P:(kc + 1) * P], rhs=hT[:, kc, n0:n0 + nl],
                                start=(kc == 0), stop=(kc == KC - 1)),
                                hdeps(hready, n0, nl) + [rdy] + list(bank_dep(bank)),
                                inc=(kc == KC - 1))
                        if bank is pA[n]:
                            tg = tk
                        else:
                            tu = tk
                    lastmm = tu
                    ti = st["tmpi"] % 2
                    st["tmpi"] += 1
                    ta = op("act", lambda e, n=n, nl=nl, ti=ti: e.activation(
                        out=tmp[:, ti, 0:nl], in_=pA[n][:, 0:nl], func=AF.Silu),
                        [tg, st["tmpfree"].get(ti)])
                    tv = op("dve", lambda e, n=n, n0=n0, nl=nl, ti=ti, jj=jj: e.tensor_tensor(
                        out=actb[:, jj, n0:n0 + nl], in0=tmp[:, ti, 0:nl], in1=pB[n][:, 0:nl], op=ALU.mult),
                        [ta, tu])
                    st["tmpfree"][ti] = tv
                    set_bank_free(pA[n], [ta])
                    set_bank_free(pB[n], [tv])
                    evs.append(tv)
                R1.release(ig, [lastmm])
                R1.release(iu, [lastmm])
            actready = evs[-1]
            lastd = None
            for nb in range(NB):
                i2, s2, r2 = R2.acquire()
                if g == NG - 1 and nb == NB - 1:
                    R2.hold = True
                for m in tiles:
                    bank = pD[st["pdi"] % 2]
                    st["pdi"] += 1
                    for jj in range(G):
                        lastd = op("pe", lambda e, bank=bank, jj=jj, m=m, s2=s2: e.matmul(
                            bank[:, :], lhsT=actb[:, jj, m * P:(m + 1) * P], rhs=w2[:, s2, jj * 512:(jj + 1) * 512],
                            start=(jj == 0), stop=(jj == G - 1)),
                            [actready, r2] + list(bank_dep(bank)), inc=(jj == G - 1))
                    tv = op("dve", lambda e, bank=bank, m=m, nb=nb: e.scalar_tensor_tensor(
                        out=xres[:, m, nb * 512:(nb + 1) * 512], in0=bank[:, :], scalar=0.5,
                        in1=xres[:, m, nb * 512:(nb + 1) * 512], op0=ALU.mult, op1=ALU.add),
                        [lastd, xdone.get((m, nb))])
                    xdone[(m, nb)] = tv
                    set_bank_free(bank, [tv])
                R2.release(i2, [lastd])
            st["act_free"] = lastd
        st["hT_free"] = lastd
        return {m: [xdone[(m, NB - 1)]] for m in tiles}

    def seg_layout(pss, S):
        if pss == 2:
            segs = [(0, T - 128, "carry"), (T - 128, 64, "in0"), (T - 64, 64, "in1")]
        else:
            segs = [(0, T, "zero" if pss == 0 else "carry")]
        out, eo = [], 0
        for (s0, ln, kd) in segs:
            out.append((s0, ln, kd, eo))
            eo += S + ln
        return out

    def pieces(n0, nl, segs):
        res = []
        for sg in segs:
            s0, ln = sg[0], sg[1]
            a, b = max(n0, s0), min(n0 + nl, s0 + ln)
            if a < b:
                res.append((a - n0, b - a, a, sg))
        return res

    def mixer(pss, hready, own):
        lo0 = (P - HALO) if pss == 0 else 0
        nts = ntiles(lo0)
        segb = seg_layout(pss, SB)
        sega = seg_layout(pss, SA)
        xdone = {}
        bg = collections.deque()
        chain_last = {}

        sub_last = {}

        def emit_bg():
            key, fn, deps0 = bg.popleft()
            tok = op("dve", fn, list(deps0) + [sub_last.get(key)])
            sub_last[key] = tok
            chain_last[key[0]] = tok

        def drain(k):
            for _ in range(min(k, len(bg))):
                emit_bg()

        def drain_chain(ch):
            while any(e_[0][0] == ch for e_ in bg):
                emit_bg()
            if ch in chain_last:
                st[("ebfree", ch % 2)] = chain_last[ch]

        def do_B(c):
            eb = c % 2
            iu, su_, ru = R1.acquire()
            ig, sg_, rg = R1.acquire()
            drain_chain(c - 2)
            ebfree = st.get(("ebfree", eb))
            stt = []
            for (s0, ln, kd, eo) in segb:
                dst = extb[:, eb, eo:eo + SB]
                if kd == "zero":
                    stt.append(op("dve", lambda e, dst=dst: e.memset(dst, 0.0), [ebfree]))
                elif kd == "carry":
                    stt.append(op("dve", lambda e, dst=dst, c=c: e.tensor_copy(out=dst, in_=stateb[:, c, :]), [ebfree]))
                else:
                    s = int(kd[-1])
                    stt.append(op("dve", lambda e, dst=dst, c=c, s=s: e.tensor_copy(out=dst, in_=sinb[:, c, s, :]),
                                  [ebfree, st.get("sin_done")]))
            lastmm = None
            glu = []
            for n, (n0, nl) in enumerate(nts):
                for (bank, slot, rdy) in ((pA[n], su_, ru), (pB[n], sg_, rg)):
                    for kc in range(KC):
                        tk = op("pe", lambda e, bank=bank, slot=slot, kc=kc, n0=n0, nl=nl: e.matmul(
                            bank[:, 0:nl], lhsT=w1[:, slot, kc * P:(kc + 1) * P], rhs=hT[:, kc, n0:n0 + nl],
                            start=(kc == 0), stop=(kc == KC - 1)),
                            hdeps(hready, n0, nl) + [rdy] + list(bank_dep(bank)), inc=(kc == KC - 1))
                    if bank is pA[n]:
                        tu_ = tk
                    else:
                        tg_ = tk
                lastmm = tg_
                ti = st["tmpi"] % 2
                st["tmpi"] += 1
                ta = op("act", lambda e, n=n, nl=nl, ti=ti: e.activation(
                    out=tmp[:, ti, 0:nl], in_=pB[n][:, 0:nl], func=AF.Sigmoid), [tg_, st["tmpfree"].get(ti)])
                tv = None
                for (bo, ln, tc, sg) in pieces(n0, nl, segb):
                    ec = sg[3] + SB + (tc - sg[0])
                    tv = op("dve", lambda e, n=n, bo=bo, ln=ln, ec=ec, eb=eb, ti=ti: e.tensor_tensor(
                        out=extb[:, eb, ec:ec + ln], in0=pA[n][:, bo:bo + ln], in1=tmp[:, ti, bo:bo + ln],
                        op=ALU.mult), [ta, tu_, ebfree] + stt)
                    glu.append(tv)
                st["tmpfree"][ti] = tv
                set_bank_free(pA[n], [tv])
                set_bank_free(pB[n], [ta])
                drain(KDRAIN)
            R1.release(iu, [lastmm])
            R1.release(ig, [lastmm])
            gl = glu[-1]
            (s0, ln, kd, eo) = segb[0]
            if pss < 2:
                op("dve", lambda e, c=c, eb=eb, eo=eo, ln=ln: e.tensor_copy(
                    out=stateb[:, c, :], in_=extb[:, eb, eo + ln:eo + ln + SB]), [gl])
            else:
                for si, (s0, ln, kd, eo) in enumerate(segb):
                    op("dve", lambda e, c=c, eb=eb, eo=eo, ln=ln, si=si: e.tensor_copy(
                        out=ostb[:, c, si * SB:(si + 1) * SB], in_=extb[:, eb, eo + ln:eo + ln + SB]), [gl])
            if pss == 2:
                e1 = segb[1][3]
                h2 = (T - 128) // 2
                cgroups = [
                    (lambda k_, eb=eb: extb[:, eb, k_:k_ + h2], convb[:, c * T:c * T + h2]),
                    (lambda k_, eb=eb, h2=h2: extb[:, eb, h2 + k_:k_ + (T - 128)], convb[:, c * T + h2:c * T + T - 128]),
                    (lambda k_, eb=eb, e1=e1: extb[:, eb, e1:e1 + 2 * (SB + 64)].rearrange(
                        "p (s w) -> p s w", w=SB + 64)[:, :, k_:k_ + 64],
                     convb[:, c * T + T - 128:c * T + T].rearrange("p (s w) -> p s w", w=64)),
                ]
            else:
                mid = lo0 + (T - lo0) // 2
                cgroups = [
                    (lambda k_, eb=eb: extb[:, eb, lo0 + k_:mid + k_], convb[:, c * T + lo0:c * T + mid]),
                    (lambda k_, eb=eb, mid=mid: extb[:, eb, mid + k_:k_ + T], convb[:, c * T + mid:(c + 1) * T]),
                ]
            subs = []
            for gi, (src, dst) in enumerate(cgroups):
                t0 = op("act", lambda e, src=src, dst=dst, c=c: e.activation(
                    out=dst, in_=src(0), func=AF.Identity, scale=cw[:, c, WA:WA + 1],
                    bias=cw[:, c, WA + WB:WA + WB + 1]), [gl, st.get("convb_free"), cst])
                subs.append([((c, gi), (lambda e, src=src, dst=dst, c=c, k_=k_: e.scalar_tensor_tensor(
                    out=dst, in0=src(k_), scalar=cw[:, c, WA + k_:WA + k_ + 1], in1=dst,
                    op0=ALU.mult, op1=ALU.add)), [t0, cst] if k_ == 1 else []) for k_ in range(1, WB)])
            for i_ in range(WB - 1):
                for sub in subs:
                    bg.append(sub[i_])
        def do_A(c):
            ea = c % 2
            iC, sC, rC = R1.acquire()
            iv, sv, rv = R1.acquire()
            iB, sBg, rB = R1.acquire()
            eafree = st.get(("eafree", ea))
            stt = []
            for (s0, ln, kd, eo) in sega:
                dst = exta[:, ea, eo:eo + SA]
                if kd == "zero":
                    stt.append(op("dve", lambda e, dst=dst: e.memset(dst, 0.0), [eafree]))
                elif kd == "carry":
                    stt.append(op("dve", lambda e, dst=dst, c=c: e.tensor_copy(out=dst, in_=statea[:, c, :]), [eafree]))
                else:
                    s = int(kd[-1])
                    stt.append(op("dve", lambda e, dst=dst, c=c, s=s: e.tensor_copy(out=dst, in_=sina[:, c, s, :]),
                                  [eafree, st.get("sin_done")]))
            lastmm = None
            tv = st.get("a_prev")
            for n, (n0, nl) in enumerate(nts):
                toks = []
                for (bank, slot, rdy) in ((pA[n], sC, rC), (pB[n], sv, rv), (pD[n], sBg, rB)):
                    for kc in range(KC):
                        tk = op("pe", lambda e, bank=bank, slot=slot, kc=kc, n0=n0, nl=nl: e.matmul(
                            bank[:, 0:nl], lhsT=w1[:, slot, kc * P:(kc + 1) * P], rhs=hT[:, kc, n0:n0 + nl],
                            start=(kc == 0), stop=(kc == KC - 1)),
                            hdeps(hready, n0, nl) + [rdy] + list(bank_dep(bank)), inc=(kc == KC - 1))
                    toks.append(tk)
                tC, tv_, tBg = toks
                lastmm = tBg
                ti = st["tmpi"] % 2
                st["tmpi"] += 1
                ta = op("act", lambda e, n=n, nl=nl, ti=ti: e.activation(
                    out=tmp[:, ti, 0:nl], in_=pA[n][:, 0:nl], func=AF.Identity), [tC, st["tmpfree"].get(ti)])
                t2i = 2 + (st["tmpi"] % 2)
                for (bo, ln, tc, sg) in pieces(n0, nl, sega):
                    ec = sg[3] + SA + (tc - sg[0])
                    t0 = op("dve", lambda e, n=n, bo=bo, ln=ln, ec=ec, ea=ea, ti=ti: e.tensor_tensor(
                        out=exta[:, ea, ec:ec + ln], in0=pB[n][:, bo:bo + ln], in1=tmp[:, ti, bo:bo + ln],
                        op=ALU.mult), [ta, tv_, eafree] + stt)
                    t1 = op("dve", lambda e, bo=bo, ln=ln, ec=ec, ea=ea, c=c, t2i=t2i: e.tensor_scalar(
                        out=tmp[:, t2i, bo:bo + ln], in0=exta[:, ea, ec - 2:ec - 2 + ln], scalar1=cw[:, c, 0:1],
                        scalar2=None, op0=ALU.mult), [t0, tv, cst])
                    for k_ in (1, 2):
                        t1 = op("dve", lambda e, bo=bo, ln=ln, ec=ec, ea=ea, c=c, t2i=t2i, k_=k_: e.scalar_tensor_tensor(
                            out=tmp[:, t2i, bo:bo + ln], in0=exta[:, ea, ec - 2 + k_:ec - 2 + k_ + ln],
                            scalar=cw[:, c, k_:k_ + 1], in1=tmp[:, t2i, bo:bo + ln], op0=ALU.mult, op1=ALU.add), [t1])
                    tv = op("dve", lambda e, n=n, bo=bo, ln=ln, tc=tc, c=c, t2i=t2i: e.tensor_tensor(
                        out=actb[:, c, tc:tc + ln], in0=tmp[:, t2i, bo:bo + ln], in1=pD[n][:, bo:bo + ln],
                        op=ALU.mult), [t1, tBg, st.get("act_free")])
                st["tmpfree"][ti] = tv
                set_bank_free(pA[n], [ta])
                set_bank_free(pB[n], [tv])
                set_bank_free(pD[n], [tv])
                drain(KDRAIN)
            for i_ in (iC, iv, iB):
                R1.release(i_, [lastmm])
            (s0, ln, kd, eo) = sega[0]
            if pss < 2:
                tv = op("dve", lambda e, c=c, ea=ea, eo=eo, ln=ln: e.tensor_copy(
                    out=statea[:, c, :], in_=exta[:, ea, eo + ln:eo + ln + SA]), [tv])
            else:
                for si, (s0, ln, kd, eo) in enumerate(sega):
                    tv = op("dve", lambda e, c=c, ea=ea, eo=eo, ln=ln, si=si: e.tensor_copy(
                        out=osta[:, c, si * SA:(si + 1) * SA], in_=exta[:, ea, eo + ln:eo + ln + SA]), [tv])
            st[("eafree", ea)] = tv
            st["a_prev"] = tv
            st["win_last"] = lastmm
        seq = [("B", 0)]
        for i_ in range(1, CB):
            seq += [("B", i_), ("A", i_ - 1)]
        seq.append(("A", CA - 1))
        for (kind, c) in seq:
            (do_B if kind == "B" else do_A)(c)
        def w_out(kh, deps, bgdrain):
            lastd = None
            for nb in range(NB):
                i2, s2, r2 = R2.acquire()
                if kh == 1 and nb == NB - 1:
                    R2.hold = True
                for m in own:
                    bank = pD[st["pdi"] % 2]
                    st["pdi"] += 1
                    for k_ in range(CA):
                        lhs = actb[:, k_, m * P:(m + 1) * P] if kh == 0 else hT[:, k_, m * P:(m + 1) * P]
                        lastd = op("pe", lambda e, bank=bank, lhs=lhs, k_=k_, s2=s2: e.matmul(
                            bank[:, :], lhsT=lhs, rhs=w2[:, s2, k_ * 512:(k_ + 1) * 512],
                            start=(k_ == 0), stop=(k_ == CA - 1)),
                            list(deps) + [r2] + list(bank_dep(bank)), inc=(k_ == CA - 1))
                    tv = op("dve", lambda e, bank=bank, m=m, nb=nb: e.tensor_tensor(
                        out=xres[:, m, nb * 512:(nb + 1) * 512], in0=bank[:, :],
                        in1=xres[:, m, nb * 512:(nb + 1) * 512], op=ALU.add), [lastd, xdone.get((m, nb))])
                    xdone[(m, nb)] = tv
                    set_bank_free(bank, [tv])
                    drain(bgdrain)
                R2.release(i2, [lastd])
            return lastd

        tS = [None, None]

        def stats_chunk(c):
            cd = chain_last[c]
            for n, (n0, nl) in enumerate(nts):
                ti = st["tmpi"] % 2
                st["tmpi"] += 1
                ta = op("act", lambda e, c=c, n0=n0, nl=nl, ti=ti: e.activation(
                    out=tmp[:, ti, 0:nl], in_=convb[:, c * T + n0:c * T + n0 + nl], func=AF.Square),
                    [cd, st["tmpfree"].get(ti)])
                op("pe", lambda e, c=c, n=n, n0=n0, nl=nl: e.matmul(
                    pA[n][:, 0:nl], lhsT=onesf[:], rhs=convb[:, c * T + n0:c * T + n0 + nl],
                    start=(c == 0), stop=(c == CB - 1)),
                    [cd, t_ones] + list(bank_dep(pA[n])), inc=False)
                tS[n] = op("pe", lambda e, c=c, n=n, nl=nl, ti=ti: e.matmul(
                    pB[n][:, 0:nl], lhsT=onesf[:], rhs=tmp[:, ti, 0:nl],
                    start=(c == 0), stop=(c == CB - 1)),
                    [ta] + list(bank_dep(pB[n])))
                st["tmpfree"][ti] = tS[n]

        n_early = max(CB - 2, 0)
        for c in range(n_early):
            drain_chain(c)
            stats_chunk(c)
        ya_done0 = st["a_prev"]
        w_out(0, [ya_done0], 6)
        drain(len(bg))
        convdone = [chain_last[c] for c in range(CB)]
        ya_done = pg.last("dve")
        win_done = st["win_last"]
        for c in range(n_early, CB):
            stats_chunk(c)
        mean = extb[:, 0, 0:T]
        rs_ = extb[:, 1, 0:T]
        msq = exta[:, 0, 0:T]
        lnready = None
        for n, (n0, nl) in enumerate(nts):
            a = op("dve", lambda e, n=n, n0=n0, nl=nl: e.tensor_scalar(
                out=mean[:, n0:n0 + nl], in0=pA[n][:, 0:nl], scalar1=1.0 / DA, scalar2=None, op0=ALU.mult),
                [tS[n], convdone[-1], ya_done])
            b = op("dve", lambda e, n0=n0, nl=nl: e.tensor_tensor(
                out=msq[:, n0:n0 + nl], in0=mean[:, n0:n0 + nl], in1=mean[:, n0:n0 + nl], op=ALU.mult), [a])
            b = op("dve", lambda e, n=n, n0=n0, nl=nl: e.scalar_tensor_tensor(
                out=msq[:, n0:n0 + nl], in0=pB[n][:, 0:nl], scalar=1.0 / DA, in1=msq[:, n0:n0 + nl],
                op0=ALU.mult, op1=ALU.subtract), [b])
            b2 = op("act", lambda e, n0=n0, nl=nl: e.activation(
                out=rs_[:, n0:n0 + nl], in_=msq[:, n0:n0 + nl], func=AF.Sqrt, bias=EPS), [b])
            lnready = op("dve", lambda e, n0=n0, nl=nl: e.reciprocal(
                out=rs_[:, n0:n0 + nl], in_=rs_[:, n0:n0 + nl]), [b2])
            set_bank_free(pA[n], [a])
            set_bank_free(pB[n], [b])
        yb = None
        lsub = {}
        for c in range(CB):
            cv = convb[:, c * T + lo0:(c + 1) * T]
            lsub[c] = op("dve", lambda e, cv=cv: e.tensor_tensor(out=cv, in0=cv, in1=mean[:, lo0:T], op=ALU.subtract),
                         [lnready])
        for c in range(CB):
            cv = convb[:, c * T + lo0:(c + 1) * T]
            a = op("dve", lambda e, cv=cv: e.tensor_tensor(out=cv, in0=cv, in1=rs_[:, lo0:T], op=ALU.mult), [lsub[c]])
            yb = op("act", lambda e, cv=cv, c=c: e.activation(
                out=hT[:, c, lo0:T], in_=cv, func=AF.Silu, scale=cw[:, c, WA + WB + 1:WA + WB + 2],
                bias=cw[:, c, WA + WB + 2:WA + WB + 3]), [a, win_done, tS[0], tS[1], cst])
        st["convb_free"] = yb
        st[("ebfree", 0)] = st[("ebfree", 1)] = yb
        st[("eafree", 0)] = st[("eafree", 1)] = pg.last("dve")
        if pss == 2:
            stg_b = convb[0:3 * SB, 0:DA]
            stg_a = convb[0:3 * SA, DA:2 * DA]
            lastcp = None
            for (src, stg, R) in ((ostb, stg_b, 3 * SB), (osta, stg_a, 3 * SA)):
                for c in range(CB):
                    bank = pA[c % 2]
                    tp = op("pe", lambda e, bank=bank, src=src, c=c, R=R: e.transpose(
                        out=bank[0:R, 0:P], in_=src[:, c, :], identity=identf[:]),
                        [pg.last("dve"), yb] + list(bank_dep(bank)))
                    lastcp = op("dve", lambda e, bank=bank, stg=stg, c=c, R=R: e.tensor_copy(
                        out=stg[:, c * P:(c + 1) * P], in_=bank[0:R, 0:P]), [tp, yb])
                    set_bank_free(bank, [lastcp])
            dma("sp", lambda e: e.dma_start(out=ob_d, in_=stg_b), "ost", [lastcp])
            dma("sp", lambda e: e.dma_start(out=oa_d, in_=stg_a), "ost", [lastcp])
        lastd = w_out(1, [yb], 0)
        st["act_free"] = lastd
        st["hT_free"] = lastd
        st["mix_done"] = [yb, pg.last("dve")]
        return {m: [xdone[(m, NB - 1)]] for m in own}

    store_tok = {}
    halo_free = []
    gf_tok = None
    for pss in range(NPASS):
        own = list(range(1, NT)) if pss == 0 else list(range(NT))
        if pss > 0:
            for m in range(NT):
                deps = [store_tok[m]] if m in store_tok else halo_free
                xload[m] = dma("sp", (lambda e, m=m, pss=pss: e.dma_start(out=xres[:, m, :], in_=xin[pss * NT + m])),
                               f"x{m}", deps)
        c0 = 0 if pss > 0 else P
        alltiles = list(range(NT))
        hg = norm_stage(alltiles, 0, lambda m: [xload[m]])
        if pss == 0:
            R2.prefill()
        xr = ffn(0, (P - HALO) if pss == 0 else 0, alltiles, hg)
        if pss == NPASS - 1:
            load_sample_states()
        hg = norm_stage(alltiles, 1, lambda m, xr=xr: xr[m])
        R2.unhold()
        if pss == 0:
            halo_free = [pg.last("dve"), pg.last("act")]
        xr2 = mixer(pss, hg, own)
        gf_tok = dma("sp", lambda e: e.dma_start(out=gfin, in_=gf_d), "gf",
                     st["mix_done"] + [st.get("sin_done"), ("ost", pg.cnt.get("ost", 0))])
        hg = norm_stage(own, 2, lambda m, xr2=xr2: xr2[m])
        R2.unhold()
        xr3 = ffn(1, c0, own, hg)
        fg = {}
        for grp in [own[i:i + 3] for i in range(0, len(own), 3)]:
            tk = rms_group(grp, lambda m, xr3=xr3: xr3[m])
            for m in grp:
                fg[m] = tk
        R2.unhold()
        for m in own:
            t3 = fg[m]
            t4 = op("dve", lambda e, m=m: e.scalar_tensor_tensor(
                out=xres[:, m, :], in0=xres[:, m, :], scalar=rstd[:, m:m + 1], in1=gfin,
                op0=ALU.mult, op1=ALU.mult), [t3, gf_tok])
            st[("rstd_free", m)] = t4
            yi = pss * NT + m - 1
            store_tok[m] = dma("sp", (lambda e, m=m, yi=yi: e.dma_start(out=y_d[yi], in_=xres[:, m, :])),
                               f"y{m}", [t4])
        st["convb_free"] = pg.last("dve")
    pg.wait("sp", list(store_tok.values()) + [("ost", pg.cnt.get("ost", 0))])

    sems = {k: es.enter_context(nc.semaphore(k)) for k in pg.semkeys}
    with es:
        with nc.Block() as block:
            def emit(e, key):
                for (fn, waits, inc) in pg.ops[key]:
                    for (k_, v) in waits:
                        e.wait_ge(sems[k_], v)
                    if fn is None:
                        continue
                    ins = fn(e)
                    if inc is not None:
                        ins.then_inc(sems[inc[0]], inc[1])

            @block.tensor
            def _(e):
                emit(e, "pe")

            @block.scalar
            def _(e):
                emit(e, "act")

            @block.vector
            def _(e):
                emit(e, "dve")

            @block.gpsimd
            def _(e):
                emit(e, "pool")

            @block.sync
            def _(e):
                emit(e, "sp")
    return nc


def prep_inputs(inp, G):
    f = lambda a: np.ascontiguousarray(np.asarray(a, dtype=np.float32))
    xp = f(inp["x_prompt"])[0]
    xs = f(inp["x_sample"])
    SEQ, D = xp.shape
    KC = D // P
    DA = D // 2
    CA = DA // P
    FF = inp["ffn1_w_gate"].shape[-1]
    FC = FF // P
    NG = FC // G
    NB = D // 512
    PT = SEQ // NCORES
    assert PT == (NPASS * NT - 2) * P and xs.shape[0] == 2 * NCORES and xs.shape[1] == 64

    def w1l(W):
        n = W.shape[1] // P
        return f(W.reshape(KC, P, n, P).transpose(2, 1, 0, 3).reshape(n, P, KC * P))

    def wdl(W):
        return f(W.reshape(NG, G, P, NB, 512).transpose(0, 3, 2, 1, 4).reshape(NG * NB, P, G * 512))

    shared = {}
    for i, k in ((1, "ffn1"), (2, "ffn2")):
        g = w1l(f(inp[f"{k}_w_gate"])[0])
        u = w1l(f(inp[f"{k}_w_up"])[0])
        shared[f"wgu{i}"] = f(np.stack([g, u], axis=1))
        shared[f"wdn{i}"] = wdl(f(inp[f"{k}_w_down"])[0])
    Win = f(inp["w_in"])[0]
    wl = w1l(Win)
    seq = [("B", 0)]
    for i_ in range(1, CA):
        seq += [("B", i_), ("A", i_ - 1)]
    seq.append(("A", CA - 1))
    order = []
    for (kind, c) in seq:
        if kind == "B":
            order += [3 * CA + c, 4 * CA + c]
        else:
            order += [CA + c, 2 * CA + c, c]
    shared["win"] = f(wl[order])
    Wo = f(inp["w_out"])[0]
    shared["wout"] = f(Wo.reshape(2, CA, P, NB, 512).transpose(3, 0, 2, 1, 4).reshape(NB * 2, P, CA * 512))
    nws = [f(inp[k])[0].reshape(KC, P).T for k in ("ffn1_norm", "mix_norm", "ffn2_norm")]
    shared["nw"] = f(np.concatenate(nws, axis=1))
    ca = f(inp["conv_a_w"])[0].reshape(WA, CA, P).transpose(2, 1, 0)
    cb = f(inp["conv_b_w"])[0].reshape(WB, CA, P).transpose(2, 1, 0)
    rest = [f(inp[k])[0].reshape(CA, P).T[:, :, None] for k in ("conv_b_bias", "conv_b_ln_g", "conv_b_ln_b")]
    shared["cw"] = f(np.concatenate([ca, cb] + rest, axis=2).reshape(P, -1))
    shared["gfin"] = f(np.broadcast_to(f(inp["final_norm"]).reshape(1, D), (P, D)))
    shared["ident"] = np.eye(P, dtype=np.float32)
    sa = f(inp["state_conv_a"])[0]
    sbb = f(inp["state_conv_b"])[0]
    maps = []
    for i in range(NCORES):
        halo = np.zeros((P, D), np.float32)
        if i > 0:
            halo[P - HALO:] = xp[i * PT - HALO:i * PT]
        xin = np.concatenate([halo[None], xp[i * PT:(i + 1) * PT].reshape(-1, P, D),
                              xs[2 * i:2 * i + 2].reshape(1, P, D)], axis=0)
        m = dict(shared)
        m["xin"] = f(xin)
        m["sa_in"] = f(sa[2 * i:2 * i + 2])
        m["sb_in"] = f(sbb[2 * i:2 * i + 2])
        maps.append(m)
    return maps, (SEQ, D, DA, PT)


def assemble(results, dims):
    SEQ, D, DA, PT = dims
    yp = np.zeros((1, SEQ, D), np.float32)
    ys = np.zeros((2 * NCORES, 64, D), np.float32)
    cas = np.zeros((1, 2 * NCORES, SA, DA), np.float32)
    cbs = np.zeros((1, 2 * NCORES, SB, DA), np.float32)
    for i, r in enumerate(results):
        y = np.asarray(r["y"], dtype=np.float32)
        yp[0, i * PT:(i + 1) * PT] = y[:-1].reshape(PT, D)
        ys[2 * i:2 * i + 2] = y[-1].reshape(2, 64, D)
        oa = np.asarray(r["oa"], dtype=np.float32).reshape(3, SA, DA)
        ob = np.asarray(r["ob"], dtype=np.float32).reshape(3, SB, DA)
        cas[0, 2 * i:2 * i + 2] = oa[1:]
        cbs[0, 2 * i:2 * i + 2] = ob[1:]
        if i == NCORES - 1:
            cap = oa[0][None, None].copy()
            cbp = ob[0][None, None].copy()
    return (yp, ys, cap, cbp, cas, cbs)


def pick_group(FC):
    for g in (11, 8, 4, 2, 1):
        if FC % g == 0:
            return g
    return 1


def kernel(**inputs):
    D = inputs["x_prompt"].shape[-1]
    FF = inputs["ffn1_w_gate"].shape[-1]
    G = pick_group(FF // P)
    maps, dims = prep_inputs(inputs, G)
    nc = build_program(D, FF, G)
    res = run_bass_kernel_spmd(nc, maps, core_ids=list(range(NCORES)))
    return assemble(res.results, dims)
```

```python
import contextlib
import collections
import numpy as np
import concourse.bass as bass
import concourse.mybir as mybir
from concourse.bass_utils import run_bass_kernel_spmd

F32 = mybir.dt.float32
BF16 = mybir.dt.bfloat16
AF = mybir.ActivationFunctionType
ALU = mybir.AluOpType
P = 128
NCORES = 8
NT = 6
NPASS = 3
T = NT * P
SA = 2
SB = 30
WA = 3
WB = 31
EPS = 1e-6
HALO = 32
NW1 = 5
NW2 = 2
KDRAIN = 16


class Prog:
    def __init__(self):
        self.ops = {k: [] for k in ("pe", "act", "dve", "pool", "sp")}
        self.cnt = {}
        self.waited = {k: {} for k in self.ops}
        self.semkeys = []

    def _sem(self, key):
        if key not in self.cnt:
            self.cnt[key] = 0
            self.semkeys.append(key)

    def op(self, eng, fn, deps=(), inc=True):
        self._sem(eng)
        waits = self._waits(eng, deps)
        if inc:
            self.cnt[eng] += 1
        self.ops[eng].append((fn, waits, (eng, 1) if inc else None))
        return (eng, self.cnt[eng])

    def dma(self, eng, fn, semkey, deps=()):
        self._sem(semkey)
        waits = self._waits(eng, deps)
        self.cnt[semkey] += 16
        self.ops[eng].append((fn, waits, (semkey, 16)))
        return (semkey, self.cnt[semkey])

    def wait(self, eng, deps):
        waits = self._waits(eng, deps)
        if waits:
            self.ops[eng].append((None, waits, None))

    def _waits(self, eng, deps):
        out = []
        w = self.waited[eng]
        for d in deps:
            if d is None:
                continue
            k, v = d
            if v <= 0:
                continue
            if eng == "pe" and k == "pe":
                continue
            if w.get(k, 0) >= v:
                continue
            w[k] = v
            out.append((k, v))
        return out

    def last(self, eng):
        return (eng, self.cnt.get(eng, 0))


class Ring:
    def __init__(self, prog, name, nslots, loads):
        self.pg, self.name, self.ns, self.loads = prog, name, nslots, loads
        self.ready, self.rel = [], {}
        self.n_issued = 0
        self.n_acq = 0

    def _issue(self, extra=()):
        i = self.n_issued
        if i >= len(self.loads):
            return
        s = i % self.ns
        deps = tuple(self.rel.get(i - self.ns, ()) if i >= self.ns else ()) + tuple(extra)
        mk = self.loads[i]
        tok = self.pg.dma("pool", (lambda e, mk=mk, s=s: mk(e, s)), f"{self.name}{s}", deps)
        self.ready.append(tok)
        self.n_issued += 1

    def prefill(self, deps=()):
        for _ in range(self.ns):
            self._issue(deps)

    def acquire(self):
        i = self.n_acq
        self.n_acq += 1
        assert i < len(self.ready), (self.name, i)
        return i, i % self.ns, self.ready[i]

    def release(self, i, toks):
        self.rel[i] = tuple(toks)
        self.to_issue = getattr(self, "to_issue", 0) + 1
        if not getattr(self, "hold", False):
            self.unhold()

    def unhold(self):
        self.hold = False
        while getattr(self, "to_issue", 0) > 0:
            self.to_issue -= 1
            self._issue()


def build_program(D, FF, G):
    KC = D // P
    FC = FF // P
    NG = FC // G
    DA = D // 2
    CA = DA // P
    CB = CA
    NB = D // 512
    NWIN = 2 * CB + 3 * CA
    NCW = WA + WB + 3
    assert CA <= G

    nc = bass.Bass("TRN2", target_bir_lowering=False)
    dt = nc.dram_tensor
    xin = dt("xin", [NPASS * NT, P, D], F32, kind="ExternalInput").ap()
    wgu = [dt(f"wgu{f}", [FC, 2, P, KC * P], F32, kind="ExternalInput").ap() for f in (1, 2)]
    wdn = [dt(f"wdn{f}", [NG * NB, P, G * 512], F32, kind="ExternalInput").ap() for f in (1, 2)]
    win = dt("win", [NWIN, P, KC * P], F32, kind="ExternalInput").ap()
    wout = dt("wout", [NB * 2, P, CA * 512], F32, kind="ExternalInput").ap()
    nw_d = dt("nw", [P, 3 * KC], F32, kind="ExternalInput").ap()
    cw_d = dt("cw", [P, CB * NCW], F32, kind="ExternalInput").ap()
    gf_d = dt("gfin", [P, D], F32, kind="ExternalInput").ap()
    id_d = dt("ident", [P, P], F32, kind="ExternalInput").ap()
    sa_d = dt("sa_in", [2, SA, DA], F32, kind="ExternalInput").ap()
    sb_d = dt("sb_in", [2, SB, DA], F32, kind="ExternalInput").ap()
    y_d = dt("y", [NPASS * NT - 1, P, D], F32, kind="ExternalOutput").ap()
    oa_d = dt("oa", [3 * SA, DA], F32, kind="ExternalOutput").ap()
    ob_d = dt("ob", [3 * SB, DA], F32, kind="ExternalOutput").ap()

    EXB = 2 * (SB + 64) + SB + (T - 128) + 6
    EXA = 2 * (SA + 64) + SA + (T - 128) + 2
    assert EXB >= SB + T and EXA >= SA + T
    CVB = max(CB * T, 4096)
    JOFF = 4 * DA
    assert JOFF + D // 2 <= CVB

    es = contextlib.ExitStack()
    sb = lambda name, shape, d: es.enter_context(nc.sbuf_tensor(name, shape, d))
    ps = lambda name, shape, d: es.enter_context(nc.psum_tensor(name, shape, d))
    xres = sb("xres", [P, NT, D], F32)
    hT = sb("hT", [P, KC, T], BF16)
    actb = sb("actb", [P, G, T], BF16)
    hb = sb("hb", [P, NT, D], BF16)
    w1 = sb("w1", [P, NW1, KC * P], BF16)
    w2 = sb("w2", [P, NW2, G * 512], BF16)
    convb = sb("convb", [P, CVB], F32)
    extb = sb("extb", [P, 2, EXB], F32)
    exta = sb("exta", [P, 2, EXA], F32)
    tmp = sb("tmp", [P, 4, 384], F32)
    nw = sb("nw_s", [P, 3 * KC], F32)
    cw = sb("cw_s", [P, CB, NCW], F32)
    identf = sb("identf", [P, P], F32)
    identb = sb("identb", [P, P], BF16)
    onesf = sb("onesf", [P, P], F32)
    ss = sb("ss", [P, NT], F32)
    rstd = sb("rstd", [P, NT], F32)
    sd = sb("sd", [P, NT], F32)
    mhalf = sb("mhalf", [P, 1], F32)
    stateb = sb("stateb", [P, CB, SB], F32)
    statea = sb("statea", [P, CA, SA], F32)
    sinb = sb("sinb", [P, CB, 2, SB], F32)
    sina = sb("sina", [P, CA, 2, SA], F32)
    ostb = sb("ostb", [P, CB, 3 * SB], F32)
    osta = sb("osta", [P, CA, 3 * SA], F32)
    pA = [ps(f"pA{i}", [P, 512], F32) for i in range(2)]
    pB = [ps(f"pB{i}", [P, 512], F32) for i in range(2)]
    pD = [ps(f"pD{i}", [P, 512], F32) for i in range(2)]
    pT = [ps(f"pT{i}", [P, 1024], BF16) for i in range(2)]

    tbanks = [pT[0], pT[1], pA[0], pA[1], pB[0], pB[1]]
    tviews = [pT[0][:], pT[1][:]] + [bk[:].bitcast(BF16) for bk in (pA[0], pA[1], pB[0], pB[1])]
    gfin = convb[:, 0:D]
    junk = convb[:, JOFF:JOFF + D // 2].bitcast(BF16)
    pg = Prog()
    op, dma = pg.op, pg.dma

    w1_loads, w2_loads = [], []

    def mk_w1(src):
        return lambda e, s: e.dma_start(out=w1[:, s, :], in_=src)

    def mk_w2(src, n):
        return lambda e, s: e.dma_start(out=w2[:, s, 0:n], in_=src)

    for p_ in range(NPASS):
        for j in range(FC):
            w1_loads += [mk_w1(wgu[0][j, 0]), mk_w1(wgu[0][j, 1])]
        for g in range(NG):
            for nb in range(NB):
                w2_loads.append(mk_w2(wdn[0][g * NB + nb], G * 512))
        for c in range(NWIN):
            w1_loads.append(mk_w1(win[c]))
        for kh in range(2):
            for nb in range(NB):
                w2_loads.append(mk_w2(wout[nb * 2 + kh], CA * 512))
        for j in range(FC):
            w1_loads += [mk_w1(wgu[1][j, 0]), mk_w1(wgu[1][j, 1])]
        for g in range(NG):
            for nb in range(NB):
                w2_loads.append(mk_w2(wdn[1][g * NB + nb], G * 512))
    R1 = Ring(pg, "w1r", NW1, w1_loads)
    R2 = Ring(pg, "w2r", NW2, w2_loads)

    c0tok = None
    for dst, src in ((nw[:], nw_d), (identf[:], id_d)):
        c0tok = dma("sp", (lambda e, dst=dst, src=src: e.dma_start(out=dst, in_=src)), "c0")
    t_mh = op("pool", lambda e: e.memset(mhalf[:], -0.5))
    xload = {}
    for m in range(3):
        xload[m] = dma("sp", (lambda e, m=m: e.dma_start(out=xres[:, m, :], in_=xin[m])), f"x{m}")
    R1._issue()
    R1._issue()
    for m in range(3, NT):
        xload[m] = dma("sp", (lambda e, m=m: e.dma_start(out=xres[:, m, :], in_=xin[m])), f"x{m}",
                       [R1.ready[0], R1.ready[1]])
    stb_stage = convb[0:SB, 0:2 * DA].rearrange("p (s d) -> p s d", s=2)
    sta_stage = convb[0:SA, 2 * DA:4 * DA].rearrange("p (s d) -> p s d", s=2)
    cst = dma("sp", lambda e: e.dma_start(out=cw[:].rearrange("p c k -> p (c k)"), in_=cw_d), "cst")
    for _ in range(NW1 - 2):
        R1._issue([xload[2]])
    t_idb = op("dve", lambda e: e.tensor_copy(out=identb[:], in_=identf[:]), [c0tok])
    t_ones = op("dve", lambda e: e.memset(onesf[:], 1.0))
    op("dve", lambda e: e.memset(extb[:], 0.0))
    op("dve", lambda e: e.memset(exta[:], 0.0))
    op("dve", lambda e: e.memset(actb[:], 0.0))

    def ntiles(c0):
        h = (T - c0) // 2
        return [(c0, h), (c0 + h, T - c0 - h)]

    st = {"pt": 0, "ptfree": {}, "bankfree": {}, "tmpfree": {}, "tmpi": 0, "pdi": 0}

    def bank_dep(b):
        return st["bankfree"].get(id(b), ())

    def set_bank_free(b, toks):
        st["bankfree"][id(b)] = tuple(toks)

    def load_sample_states():
        free = [st.get("convb_free")]
        dma("sp", lambda e: e.dma_start(out=stb_stage, in_=sb_d.rearrange("s r d -> r s d")), "sst", free)
        ld = dma("sp", lambda e: e.dma_start(out=sta_stage, in_=sa_d.rearrange("s r d -> r s d")), "sst", free)
        k = 0
        for (stage, dstt, R) in ((stb_stage, sinb, SB), (sta_stage, sina, SA)):
            for s in range(2):
                for c in range(CB):
                    bank = pD[k % 2]
                    k += 1
                    tp = op("pe", lambda e, bank=bank, stage=stage, s=s, c=c, R=R: e.transpose(
                        out=bank[:, 0:R], in_=stage[:, s, c * P:(c + 1) * P], identity=identf[0:R, 0:R]),
                        [ld, c0tok] + list(bank_dep(bank)))
                    cp = op("dve", lambda e, bank=bank, dstt=dstt, s=s, c=c, R=R: e.tensor_copy(
                        out=dstt[:, c, s, :], in_=bank[:, 0:R]), [tp])
                    set_bank_free(bank, [cp])
        st["sin_done"] = pg.last("dve")

    def rms_group(grp, xready):
        lo, hi = grp[0], grp[-1] + 1
        t1 = None
        for m in grp:
            t1 = op("act", lambda e, m=m: e.activation(out=junk, in_=xres[:, m, :], func=AF.Square,
                                                       scale=float(D) ** -0.5, accum_out=ss[:, m:m + 1]),
                    list(xready(m)) + [st.get("junk_prev"), st.get(("ss_free", m))])
            st["junk_prev"] = t1
        t2 = op("pool", lambda e: e.tensor_scalar(out=sd[:, lo:hi], in0=ss[:, lo:hi], scalar1=EPS,
                                                  scalar2=1.0, op0=ALU.add, op1=ALU.mult),
                [t1] + [st.get(("sd_free", m)) for m in grp])
        t3 = op("pool", lambda e: e.tensor_tensor(out=rstd[:, lo:hi], in0=sd[:, lo:hi],
                                                  in1=mhalf[:, 0:1].to_broadcast([P, hi - lo]), op=ALU.pow),
                [t2, t_mh] + [st.get(("rstd_free", m)) for m in grp])
        for m in grp:
            st[("ss_free", m)] = t2
            st[("sd_free", m)] = t3
        return t3

    def norm_stage(tiles, which, xready):
        groups = [tiles[i:i + 3] for i in range(0, len(tiles), 3)]
        res = []
        rdy = {}
        for grp in groups:
            tk = rms_group(grp, xready)
            for m in grp:
                rdy[m] = tk
        for grp in groups:
            hts = []
            for m in grp:
                t3 = rdy[m]
                t4 = op("dve", lambda e, m=m: e.tensor_scalar(out=hb[:, m, :], in0=xres[:, m, :],
                                                              scalar1=rstd[:, m:m + 1], scalar2=None, op0=ALU.mult),
                        [t3, st.get(("hb_free", m))])
                st[("rstd_free", m)] = t4
                hts.append(t4)
            col0 = grp[0] * P
            w = len(grp) * P
            lastpe = None
            evs = {}
            for kc in range(KC):
                b = st["pt"] % len(tbanks)
                st["pt"] += 1
                bank = tbanks[b]
                buf = tviews[b][:, 0:w]
                for i, m in enumerate(grp):
                    lastpe = op("pe", lambda e, buf=buf, i=i, m=m, kc=kc: e.transpose(
                        out=buf[:, i * P:(i + 1) * P], in_=hb[:, m, kc * P:(kc + 1) * P], identity=identb[:]),
                        hts + list(bank_dep(bank)) + [t_idb], inc=(i == len(grp) - 1))
                sc = nw[:, which * KC + kc:which * KC + kc + 1]
                if kc % 2 == 0:
                    ev = op("dve", lambda e, buf=buf, kc=kc, sc=sc, col0=col0, w=w: e.tensor_scalar(
                        out=hT[:, kc, col0:col0 + w], in0=buf, scalar1=sc, scalar2=None, op0=ALU.mult),
                        [lastpe, st.get("hT_free"), c0tok])
                    evs["dve"] = ev
                else:
                    ev = op("act", lambda e, buf=buf, kc=kc, sc=sc, col0=col0, w=w: e.activation(
                        out=hT[:, kc, col0:col0 + w], in_=buf, func=AF.Identity, scale=sc),
                        [lastpe, st.get("hT_free"), c0tok])
                    evs["act"] = ev
                set_bank_free(bank, [ev])
            for m in grp:
                st[("hb_free", m)] = lastpe
            res.append((col0, col0 + w, list(evs.values())))
        return res

    def hdeps(hgroups, n0, nl):
        out = []
        for (lo, hi, toks) in hgroups:
            if lo < n0 + nl and hi > n0:
                out += toks
        return out

    def ffn(f, c0, tiles, hready):
        nts = ntiles(c0)
        xdone = {}
        for g in range(NG):
            evs = []
            for jj in range(G):
                ig, sg_, rg = R1.acquire()
                iu, su_, ru = R1.acquire()
                lastmm = None
                for n, (n0, nl) in enumerate(nts):
                    tg = tu = None
                    for (bank, slot, rdy) in ((pA[n], sg_, rg), (pB[n], su_, ru)):
                        for kc in range(KC):
                            tk = op("pe", lambda e, bank=bank, slot=slot, kc=kc, n0=n0, nl=nl: e.matmul(
                                bank[:, 0:nl], lhsT=w1[:, slot, kc * P:(kc + 1) * P], rhs=hT[:, kc, n0:n0 + nl],
                                start=(kc == 0), stop=(kc == KC - 1)),
                                hdeps(hready, n0, nl) + [rdy] + list(bank_dep(bank)),
                                inc=(kc == KC - 1))
                        if bank is pA[n]:
                            tg = tk
                        else:
                            tu = tk
                    lastmm = tu
                    ti = st["tmpi"] % 2
                    st["tmpi"] += 1
                    ta = op("act", lambda e, n=n, nl=nl, ti=ti: e.activation(
                        out=tmp[:, ti, 0:nl], in_=pA[n][:, 0:nl], func=AF.Silu),
                        [tg, st["tmpfree"].get(ti)])
                    tv = op("dve", lambda e, n=n, n0=n0, nl=nl, ti=ti, jj=jj: e.tensor_tensor(
                        out=actb[:, jj, n0:n0 + nl], in0=tmp[:, ti, 0:nl], in1=pB[n][:, 0:nl], op=ALU.mult),
                        [ta, tu])
                    st["tmpfree"][ti] = tv
                    set_bank_free(pA[n], [ta])
                    set_bank_free(pB[n], [tv])
                    evs.append(tv)
                R1.release(ig, [lastmm])
                R1.release(iu, [lastmm])
            actready = evs[-1]
            lastd = None
            for nb in range(NB):
                i2, s2, r2 = R2.acquire()
                if g == NG - 1 and nb == NB - 1:
                    R2.hold = True
                for m in tiles:
                    bank = pD[st["pdi"] % 2]
                    st["pdi"] += 1
                    for jj in range(G):
                        lastd = op("pe", lambda e, bank=bank, jj=jj, m=m, s2=s2: e.matmul(
                            bank[:, :], lhsT=actb[:, jj, m * P:(m + 1) * P], rhs=w2[:, s2, jj * 512:(jj + 1) * 512],
                            start=(jj == 0), stop=(jj == G - 1)),
                            [actready, r2] + list(bank_dep(bank)), inc=(jj == G - 1))
                    tv = op("dve", lambda e, bank=bank, m=m, nb=nb: e.scalar_tensor_tensor(
                        out=xres[:, m, nb * 512:(nb + 1) * 512], in0=bank[:, :], scalar=0.5,
                        in1=xres[:, m, nb * 512:(nb + 1) * 512], op0=ALU.mult, op1=ALU.add),
                        [lastd, xdone.get((m, nb))])
                    xdone[(m, nb)] = tv
                    set_bank_free(bank, [tv])
                R2.release(i2, [lastd])
            st["act_free"] = lastd
        st["hT_free"] = lastd
        return {m: [xdone[(m, NB - 1)]] for m in tiles}

    def seg_layout(pss, S):
        if pss == 2:
            segs = [(0, T - 128, "carry"), (T - 128, 64, "in0"), (T - 64, 64, "in1")]
        else:
            segs = [(0, T, "zero" if pss == 0 else "carry")]
        out, eo = [], 0
        for (s0, ln, kd) in segs:
            out.append((s0, ln, kd, eo))
            eo += S + ln
        return out

    def pieces(n0, nl, segs):
        res = []
        for sg in segs:
            s0, ln = sg[0], sg[1]
            a, b = max(n0, s0), min(n0 + nl, s0 + ln)
            if a < b:
                res.append((a - n0, b - a, a, sg))
        return res

    def mixer(pss, hready, own):
        lo0 = (P - HALO) if pss == 0 else 0
        nts = ntiles(lo0)
        segb = seg_layout(pss, SB)
        sega = seg_layout(pss, SA)
        xdone = {}
        bg = collections.deque()
        chain_last = {}

        sub_last = {}

        def emit_bg():
            key, fn, deps0 = bg.popleft()
            tok = op("dve", fn, list(deps0) + [sub_last.get(key)])
            sub_last[key] = tok
            chain_last[key[0]] = tok

        def drain(k):
            for _ in range(min(k, len(bg))):
                emit_bg()

        def drain_chain(ch):
            while any(e_[0][0] == ch for e_ in bg):
                emit_bg()
            if ch in chain_last:
                st[("ebfree", ch % 2)] = chain_last[ch]

        def do_B(c):
            eb = c % 2
            iu, su_, ru = R1.acquire()
            ig, sg_, rg = R1.acquire()
            drain_chain(c - 2)
            ebfree = st.get(("ebfree", eb))
            stt = []
            for (s0, ln, kd, eo) in segb:
                dst = extb[:, eb, eo:eo + SB]
                if kd == "zero":
                    stt.append(op("dve", lambda e, dst=dst: e.memset(dst, 0.0), [ebfree]))
                elif kd == "carry":
                    stt.append(op("dve", lambda e, dst=dst, c=c: e.tensor_copy(out=dst, in_=stateb[:, c, :]), [ebfree]))
                else:
                    s = int(kd[-1])
                    stt.append(op("dve", lambda e, dst=dst, c=c, s=s: e.tensor_copy(out=dst, in_=sinb[:, c, s, :]),
                                  [ebfree, st.get("sin_done")]))
            lastmm = None
            glu = []
            for n, (n0, nl) in enumerate(nts):
                for (bank, slot, rdy) in ((pA[n], su_, ru), (pB[n], sg_, rg)):
                    for kc in range(KC):
                        tk = op("pe", lambda e, bank=bank, slot=slot, kc=kc, n0=n0, nl=nl: e.matmul(
                            bank[:, 0:nl], lhsT=w1[:, slot, kc * P:(kc + 1) * P], rhs=hT[:, kc, n0:n0 + nl],
                            start=(kc == 0), stop=(kc == KC - 1)),
                            hdeps(hready, n0, nl) + [rdy] + list(bank_dep(bank)), inc=(kc == KC - 1))
                    if bank is pA[n]:
                        tu_ = tk
                    else:
                        tg_ = tk
                lastmm = tg_
                ti = st["tmpi"] % 2
                st["tmpi"] += 1
                ta = op("act", lambda e, n=n, nl=nl, ti=ti: e.activation(
                    out=tmp[:, ti, 0:nl], in_=pB[n][:, 0:nl], func=AF.Sigmoid), [tg_, st["tmpfree"].get(ti)])
                tv = None
                for (bo, ln, tc, sg) in pieces(n0, nl, segb):
                    ec = sg[3] + SB + (tc - sg[0])
                    tv = op("dve", lambda e, n=n, bo=bo, ln=ln, ec=ec, eb=eb, ti=ti: e.tensor_tensor(
                        out=extb[:, eb, ec:ec + ln], in0=pA[n][:, bo:bo + ln], in1=tmp[:, ti, bo:bo + ln],
                        op=ALU.mult), [ta, tu_, ebfree] + stt)
                    glu.append(tv)
                st["tmpfree"][ti] = tv
                set_bank_free(pA[n], [tv])
                set_bank_free(pB[n], [ta])
                drain(KDRAIN)
            R1.release(iu, [lastmm])
            R1.release(ig, [lastmm])
            gl = glu[-1]
            (s0, ln, kd, eo) = segb[0]
            if pss < 2:
                op("dve", lambda e, c=c, eb=eb, eo=eo, ln=ln: e.tensor_copy(
                    out=stateb[:, c, :], in_=extb[:, eb, eo + ln:eo + ln + SB]), [gl])
            else:
                for si, (s0, ln, kd, eo) in enumerate(segb):
                    op("dve", lambda e, c=c, eb=eb, eo=eo, ln=ln, si=si: e.tensor_copy(
                        out=ostb[:, c, si * SB:(si + 1) * SB], in_=extb[:, eb, eo + ln:eo + ln + SB]), [gl])
            if pss == 2:
                e1 = segb[1][3]
                h2 = (T - 128) // 2
                cgroups = [
                    (lambda k_, eb=eb: extb[:, eb, k_:k_ + h2], convb[:, c * T:c * T + h2]),
                    (lambda k_, eb=eb, h2=h2: extb[:, eb, h2 + k_:k_ + (T - 128)], convb[:, c * T + h2:c * T + T - 128]),
                    (lambda k_, eb=eb, e1=e1: extb[:, eb, e1:e1 + 2 * (SB + 64)].rearrange(
                        "p (s w) -> p s w", w=SB + 64)[:, :, k_:k_ + 64],
                     convb[:, c * T + T - 128:c * T + T].rearrange("p (s w) -> p s w", w=64)),
                ]
            else:
                mid = lo0 + (T - lo0) // 2
                cgroups = [
                    (lambda k_, eb=eb: extb[:, eb, lo0 + k_:mid + k_], convb[:, c * T + lo0:c * T + mid]),
                    (lambda k_, eb=eb, mid=mid: extb[:, eb, mid + k_:k_ + T], convb[:, c * T + mid:(c + 1) * T]),
                ]
            subs = []
            for gi, (src, dst) in enumerate(cgroups):
                t0 = op("act", lambda e, src=src, dst=dst, c=c: e.activation(
                    out=dst, in_=src(0), func=AF.Identity, scale=cw[:, c, WA:WA + 1],
                    bias=cw[:, c, WA + WB:WA + WB + 1]), [gl, st.get("convb_free"), cst])
                subs.append([((c, gi), (lambda e, src=src, dst=dst, c=c, k_=k_: e.scalar_tensor_tensor(
                    out=dst, in0=src(k_), scalar=cw[:, c, WA + k_:WA + k_ + 1], in1=dst,
                    op0=ALU.mult, op1=ALU.add)), [t0, cst] if k_ == 1 else []) for k_ in range(1, WB)])
            for i_ in range(WB - 1):
                for sub in subs:
                    bg.append(sub[i_])
        def do_A(c):
            ea = c % 2
            iC, sC, rC = R1.acquire()
            iv, sv, rv = R1.acquire()
            iB, sBg, rB = R1.acquire()
            eafree = st.get(("eafree", ea))
            stt = []
            for (s0, ln, kd, eo) in sega:
                dst = exta[:, ea, eo:eo + SA]
                if kd == "zero":
                    stt.append(op("dve", lambda e, dst=dst: e.memset(dst, 0.0), [eafree]))
                elif kd == "carry":
                    stt.append(op("dve", lambda e, dst=dst, c=c: e.tensor_copy(out=dst, in_=statea[:, c, :]), [eafree]))
                else:
                    s = int(kd[-1])
                    stt.append(op("dve", lambda e, dst=dst, c=c, s=s: e.tensor_copy(out=dst, in_=sina[:, c, s, :]),
                                  [eafree, st.get("sin_done")]))
            lastmm = None
            tv = st.get("a_prev")
            for n, (n0, nl) in enumerate(nts):
                toks = []
                for (bank, slot, rdy) in ((pA[n], sC, rC), (pB[n], sv, rv), (pD[n], sBg, rB)):
                    for kc in range(KC):
                        tk = op("pe", lambda e, bank=bank, slot=slot, kc=kc, n0=n0, nl=nl: e.matmul(
                            bank[:, 0:nl], lhsT=w1[:, slot, kc * P:(kc + 1) * P], rhs=hT[:, kc, n0:n0 + nl],
                            start=(kc == 0), stop=(kc == KC - 1)),
                            hdeps(hready, n0, nl) + [rdy] + list(bank_dep(bank)), inc=(kc == KC - 1))
                    toks.append(tk)
                tC, tv_, tBg = toks
                lastmm = tBg
                ti = st["tmpi"] % 2
                st["tmpi"] += 1
                ta = op("act", lambda e, n=n, nl=nl, ti=ti: e.activation(
                    out=tmp[:, ti, 0:nl], in_=pA[n][:, 0:nl], func=AF.Identity), [tC, st["tmpfree"].get(ti)])
                t2i = 2 + (st["tmpi"] % 2)
                nfill = [0]
                for (bo, ln, tc, sg) in pieces(n0, nl, sega):
                    ec = sg[3] + SA + (tc - sg[0])
                    t0 = op("dve", lambda e, n=n, bo=bo, ln=ln, ec=ec, ea=ea, ti=ti: e.tensor_tensor(
                        out=exta[:, ea, ec:ec + ln], in0=pB[n][:, bo:bo + ln], in1=tmp[:, ti, bo:bo + ln],
                        op=ALU.mult), [ta, tv_, eafree] + stt)
                    drain(1)
                    nfill[0] += 1
                    t1 = op("dve", lambda e, bo=bo, ln=ln, ec=ec, ea=ea, c=c, t2i=t2i: e.tensor_scalar(
                        out=tmp[:, t2i, bo:bo + ln], in0=exta[:, ea, ec - 2:ec - 2 + ln], scalar1=cw[:, c, 0:1],
                        scalar2=None, op0=ALU.mult), [t0, tv, cst])
                    drain(1)
                    nfill[0] += 1
                    for k_ in (1, 2):
                        t1 = op("dve", lambda e, bo=bo, ln=ln, ec=ec, ea=ea, c=c, t2i=t2i, k_=k_: e.scalar_tensor_tensor(
                            out=tmp[:, t2i, bo:bo + ln], in0=exta[:, ea, ec - 2 + k_:ec - 2 + k_ + ln],
                            scalar=cw[:, c, k_:k_ + 1], in1=tmp[:, t2i, bo:bo + ln], op0=ALU.mult, op1=ALU.add), [t1])
                        drain(1)
                        nfill[0] += 1
                    tv = op("dve", lambda e, n=n, bo=bo, ln=ln, tc=tc, c=c, t2i=t2i: e.tensor_tensor(
                        out=actb[:, c, tc:tc + ln], in0=tmp[:, t2i, bo:bo + ln], in1=pD[n][:, bo:bo + ln],
                        op=ALU.mult), [t1, tBg, st.get("act_free")])
                st["tmpfree"][ti] = tv
                set_bank_free(pA[n], [ta])
                set_bank_free(pB[n], [tv])
                set_bank_free(pD[n], [tv])
                drain(max(KDRAIN - nfill[0], 0))
            for i_ in (iC, iv, iB):
                R1.release(i_, [lastmm])
            (s0, ln, kd, eo) = sega[0]
            if pss < 2:
                tv = op("dve", lambda e, c=c, ea=ea, eo=eo, ln=ln: e.tensor_copy(
                    out=statea[:, c, :], in_=exta[:, ea, eo + ln:eo + ln + SA]), [tv])
            else:
                for si, (s0, ln, kd, eo) in enumerate(sega):
                    tv = op("dve", lambda e, c=c, ea=ea, eo=eo, ln=ln, si=si: e.tensor_copy(
                        out=osta[:, c, si * SA:(si + 1) * SA], in_=exta[:, ea, eo + ln:eo + ln + SA]), [tv])
            st[("eafree", ea)] = tv
            st["a_prev"] = tv
            st["win_last"] = lastmm
        seq = [("B", 0)]
        for i_ in range(1, CB):
            seq += [("B", i_), ("A", i_ - 1)]
        seq.append(("A", CA - 1))
        for (kind, c) in seq:
            (do_B if kind == "B" else do_A)(c)
        def w_out(kh, deps, bgdrain):
            lastd = None
            for nb in range(NB):
                i2, s2, r2 = R2.acquire()
                if kh == 1 and nb == NB - 1:
                    R2.hold = True
                for m in own:
                    bank = pD[st["pdi"] % 2]
                    st["pdi"] += 1
                    for k_ in range(CA):
                        lhs = actb[:, k_, m * P:(m + 1) * P] if kh == 0 else hT[:, k_, m * P:(m + 1) * P]
                        lastd = op("pe", lambda e, bank=bank, lhs=lhs, k_=k_, s2=s2: e.matmul(
                            bank[:, :], lhsT=lhs, rhs=w2[:, s2, k_ * 512:(k_ + 1) * 512],
                            start=(k_ == 0), stop=(k_ == CA - 1)),
                            list(deps) + [r2] + list(bank_dep(bank)), inc=(k_ == CA - 1))
                    tv = op("dve", lambda e, bank=bank, m=m, nb=nb: e.tensor_tensor(
                        out=xres[:, m, nb * 512:(nb + 1) * 512], in0=bank[:, :],
                        in1=xres[:, m, nb * 512:(nb + 1) * 512], op=ALU.add), [lastd, xdone.get((m, nb))])
                    xdone[(m, nb)] = tv
                    set_bank_free(bank, [tv])
                    drain(bgdrain)
                R2.release(i2, [lastd])
            return lastd

        tS = [None, None]

        def stats_chunk(c):
            cd = chain_last[c]
            for n, (n0, nl) in enumerate(nts):
                ti = st["tmpi"] % 2
                st["tmpi"] += 1
                ta = op("act", lambda e, c=c, n0=n0, nl=nl, ti=ti: e.activation(
                    out=tmp[:, ti, 0:nl], in_=convb[:, c * T + n0:c * T + n0 + nl], func=AF.Square),
                    [cd, st["tmpfree"].get(ti)])
                op("pe", lambda e, c=c, n=n, n0=n0, nl=nl: e.matmul(
                    pA[n][:, 0:nl], lhsT=onesf[:], rhs=convb[:, c * T + n0:c * T + n0 + nl],
                    start=(c == 0), stop=(c == CB - 1)),
                    [cd, t_ones] + list(bank_dep(pA[n])), inc=False)
                tS[n] = op("pe", lambda e, c=c, n=n, nl=nl, ti=ti: e.matmul(
                    pB[n][:, 0:nl], lhsT=onesf[:], rhs=tmp[:, ti, 0:nl],
                    start=(c == 0), stop=(c == CB - 1)),
                    [ta] + list(bank_dep(pB[n])))
                st["tmpfree"][ti] = tS[n]

        n_early = max(CB - 2, 0)
        for c in range(n_early):
            drain_chain(c)
            stats_chunk(c)
        ya_done0 = st["a_prev"]
        w_out(0, [ya_done0], 6)
        drain(len(bg))
        convdone = [chain_last[c] for c in range(CB)]
        ya_done = pg.last("dve")
        win_done = st["win_last"]
        for c in range(n_early, CB):
            stats_chunk(c)
        mean = extb[:, 0, 0:T]
        rs_ = extb[:, 1, 0:T]
        msq = exta[:, 0, 0:T]
        lnready = None
        for n, (n0, nl) in enumerate(nts):
            a = op("dve", lambda e, n=n, n0=n0, nl=nl: e.tensor_scalar(
                out=mean[:, n0:n0 + nl], in0=pA[n][:, 0:nl], scalar1=1.0 / DA, scalar2=None, op0=ALU.mult),
                [tS[n], convdone[-1], ya_done])
            b = op("dve", lambda e, n0=n0, nl=nl: e.tensor_tensor(
                out=msq[:, n0:n0 + nl], in0=mean[:, n0:n0 + nl], in1=mean[:, n0:n0 + nl], op=ALU.mult), [a])
            b = op("dve", lambda e, n=n, n0=n0, nl=nl: e.scalar_tensor_tensor(
                out=msq[:, n0:n0 + nl], in0=pB[n][:, 0:nl], scalar=1.0 / DA, in1=msq[:, n0:n0 + nl],
                op0=ALU.mult, op1=ALU.subtract), [b])
            b2 = op("act", lambda e, n0=n0, nl=nl: e.activation(
                out=rs_[:, n0:n0 + nl], in_=msq[:, n0:n0 + nl], func=AF.Sqrt, bias=EPS), [b])
            lnready = op("dve", lambda e, n0=n0, nl=nl: e.reciprocal(
                out=rs_[:, n0:n0 + nl], in_=rs_[:, n0:n0 + nl]), [b2])
            set_bank_free(pA[n], [a])
            set_bank_free(pB[n], [b])
        yb = None
        lsub = {}
        for c in range(CB):
            cv = convb[:, c * T + lo0:(c + 1) * T]
            lsub[c] = op("dve", lambda e, cv=cv: e.tensor_tensor(out=cv, in0=cv, in1=mean[:, lo0:T], op=ALU.subtract),
                         [lnready])
        for c in range(CB):
            cv = convb[:, c * T + lo0:(c + 1) * T]
            a = op("dve", lambda e, cv=cv: e.tensor_tensor(out=cv, in0=cv, in1=rs_[:, lo0:T], op=ALU.mult), [lsub[c]])
            yb = op("act", lambda e, cv=cv, c=c: e.activation(
                out=hT[:, c, lo0:T], in_=cv, func=AF.Silu, scale=cw[:, c, WA + WB + 1:WA + WB + 2],
                bias=cw[:, c, WA + WB + 2:WA + WB + 3]), [a, win_done, tS[0], tS[1], cst])
        st["convb_free"] = yb
        st[("ebfree", 0)] = st[("ebfree", 1)] = yb
        st[("eafree", 0)] = st[("eafree", 1)] = pg.last("dve")
        if pss == 2:
            stg_b = convb[0:3 * SB, 0:DA]
            stg_a = convb[0:3 * SA, DA:2 * DA]
            lastcp = None
            for (src, stg, R) in ((ostb, stg_b, 3 * SB), (osta, stg_a, 3 * SA)):
                for c in range(CB):
                    bank = pA[c % 2]
                    tp = op("pe", lambda e, bank=bank, src=src, c=c, R=R: e.transpose(
                        out=bank[0:R, 0:P], in_=src[:, c, :], identity=identf[:]),
                        [pg.last("dve"), yb] + list(bank_dep(bank)))
                    lastcp = op("dve", lambda e, bank=bank, stg=stg, c=c, R=R: e.tensor_copy(
                        out=stg[:, c * P:(c + 1) * P], in_=bank[0:R, 0:P]), [tp, yb])
                    set_bank_free(bank, [lastcp])
            dma("sp", lambda e: e.dma_start(out=ob_d, in_=stg_b), "ost", [lastcp])
            dma("sp", lambda e: e.dma_start(out=oa_d, in_=stg_a), "ost", [lastcp])
        lastd = w_out(1, [yb], 0)
        st["act_free"] = lastd
        st["hT_free"] = lastd
        st["mix_done"] = [yb, pg.last("dve")]
        return {m: [xdone[(m, NB - 1)]] for m in own}

    store_tok = {}
    halo_free = []
    gf_tok = None
    for pss in range(NPASS):
        own = list(range(1, NT)) if pss == 0 else list(range(NT))
        if pss > 0:
            for m in range(NT):
                deps = [store_tok[m]] if m in store_tok else halo_free
                xload[m] = dma("sp", (lambda e, m=m, pss=pss: e.dma_start(out=xres[:, m, :], in_=xin[pss * NT + m])),
                               f"x{m}", deps)
        c0 = 0 if pss > 0 else P
        alltiles = list(range(NT))
        hg = norm_stage(alltiles, 0, lambda m: [xload[m]])
        if pss == 0:
            R2.prefill()
        xr = ffn(0, (P - HALO) if pss == 0 else 0, alltiles, hg)
        if pss == NPASS - 1:
            load_sample_states()
        hg = norm_stage(alltiles, 1, lambda m, xr=xr: xr[m])
        R2.unhold()
        if pss == 0:
            halo_free = [pg.last("dve"), pg.last("act")]
        xr2 = mixer(pss, hg, own)
        gf_tok = dma("sp", lambda e: e.dma_start(out=gfin, in_=gf_d), "gf",
                     st["mix_done"] + [st.get("sin_done"), ("ost", pg.cnt.get("ost", 0))])
        hg = norm_stage(own, 2, lambda m, xr2=xr2: xr2[m])
        R2.unhold()
        xr3 = ffn(1, c0, own, hg)
        fg = {}
        for grp in [own[i:i + 3] for i in range(0, len(own), 3)]:
            tk = rms_group(grp, lambda m, xr3=xr3: xr3[m])
            for m in grp:
                fg[m] = tk
        R2.unhold()
        for m in own:
            t3 = fg[m]
            t4 = op("dve", lambda e, m=m: e.scalar_tensor_tensor(
                out=xres[:, m, :], in0=xres[:, m, :], scalar=rstd[:, m:m + 1], in1=gfin,
                op0=ALU.mult, op1=ALU.mult), [t3, gf_tok])
            st[("rstd_free", m)] = t4
            yi = pss * NT + m - 1
            store_tok[m] = dma("sp", (lambda e, m=m, yi=yi: e.dma_start(out=y_d[yi], in_=xres[:, m, :])),
                               f"y{m}", [t4])
        st["convb_free"] = pg.last("dve")
    pg.wait("sp", list(store_tok.values()) + [("ost", pg.cnt.get("ost", 0))])

    sems = {k: es.enter_context(nc.semaphore(k)) for k in pg.semkeys}
    with es:
        with nc.Block() as block:
            def emit(e, key):
                for (fn, waits, inc) in pg.ops[key]:
                    for (k_, v) in waits:
                        e.wait_ge(sems[k_], v)
                    if fn is None:
                        continue
                    ins = fn(e)
                    if inc is not None:
                        ins.then_inc(sems[inc[0]], inc[1])

            @block.tensor
            def _(e):
                emit(e, "pe")

            @block.scalar
            def _(e):
                emit(e, "act")

            @block.vector
            def _(e):
                emit(e, "dve")

            @block.gpsimd
            def _(e):
                emit(e, "pool")

            @block.sync
            def _(e):
                emit(e, "sp")
    return nc


def prep_inputs(inp, G):
    f = lambda a: np.ascontiguousarray(np.asarray(a, dtype=np.float32))
    xp = f(inp["x_prompt"])[0]
    xs = f(inp["x_sample"])
    SEQ, D = xp.shape
    KC = D // P
    DA = D // 2
    CA = DA // P
    FF = inp["ffn1_w_gate"].shape[-1]
    FC = FF // P
    NG = FC // G
    NB = D // 512
    PT = SEQ // NCORES
    assert PT == (NPASS * NT - 2) * P and xs.shape[0] == 2 * NCORES and xs.shape[1] == 64

    def w1l(W):
        n = W.shape[1] // P
        return f(W.reshape(KC, P, n, P).transpose(2, 1, 0, 3).reshape(n, P, KC * P))

    def wdl(W):
        return f(W.reshape(NG, G, P, NB, 512).transpose(0, 3, 2, 1, 4).reshape(NG * NB, P, G * 512))

    shared = {}
    for i, k in ((1, "ffn1"), (2, "ffn2")):
        g = w1l(f(inp[f"{k}_w_gate"])[0])
        u = w1l(f(inp[f"{k}_w_up"])[0])
        shared[f"wgu{i}"] = f(np.stack([g, u], axis=1))
        shared[f"wdn{i}"] = wdl(f(inp[f"{k}_w_down"])[0])
    Win = f(inp["w_in"])[0]
    wl = w1l(Win)
    seq = [("B", 0)]
    for i_ in range(1, CA):
        seq += [("B", i_), ("A", i_ - 1)]
    seq.append(("A", CA - 1))
    order = []
    for (kind, c) in seq:
        if kind == "B":
            order += [3 * CA + c, 4 * CA + c]
        else:
            order += [CA + c, 2 * CA + c, c]
    shared["win"] = f(wl[order])
    Wo = f(inp["w_out"])[0]
    shared["wout"] = f(Wo.reshape(2, CA, P, NB, 512).transpose(3, 0, 2, 1, 4).reshape(NB * 2, P, CA * 512))
    nws = [f(inp[k])[0].reshape(KC, P).T for k in ("ffn1_norm", "mix_norm", "ffn2_norm")]
    shared["nw"] = f(np.concatenate(nws, axis=1))
    ca = f(inp["conv_a_w"])[0].reshape(WA, CA, P).transpose(2, 1, 0)
    cb = f(inp["conv_b_w"])[0].reshape(WB, CA, P).transpose(2, 1, 0)
    rest = [f(inp[k])[0].reshape(CA, P).T[:, :, None] for k in ("conv_b_bias", "conv_b_ln_g", "conv_b_ln_b")]
    shared["cw"] = f(np.concatenate([ca, cb] + rest, axis=2).reshape(P, -1))
    shared["gfin"] = f(np.broadcast_to(f(inp["final_norm"]).reshape(1, D), (P, D)))
    shared["ident"] = np.eye(P, dtype=np.float32)
    sa = f(inp["state_conv_a"])[0]
    sbb = f(inp["state_conv_b"])[0]
    maps = []
    for i in range(NCORES):
        halo = np.zeros((P, D), np.float32)
        if i > 0:
            halo[P - HALO:] = xp[i * PT - HALO:i * PT]
        xin = np.concatenate([halo[None], xp[i * PT:(i + 1) * PT].reshape(-1, P, D),
                              xs[2 * i:2 * i + 2].reshape(1, P, D)], axis=0)
        m = dict(shared)
        m["xin"] = f(xin)
        m["sa_in"] = f(sa[2 * i:2 * i + 2])
        m["sb_in"] = f(sbb[2 * i:2 * i + 2])
        maps.append(m)
    return maps, (SEQ, D, DA, PT)


def assemble(results, dims):
    SEQ, D, DA, PT = dims
    yp = np.zeros((1, SEQ, D), np.float32)
    ys = np.zeros((2 * NCORES, 64, D), np.float32)
    cas = np.zeros((1, 2 * NCORES, SA, DA), np.float32)
    cbs = np.zeros((1, 2 * NCORES, SB, DA), np.float32)
    for i, r in enumerate(results):
        y = np.asarray(r["y"], dtype=np.float32)
        yp[0, i * PT:(i + 1) * PT] = y[:-1].reshape(PT, D)
        ys[2 * i:2 * i + 2] = y[-1].reshape(2, 64, D)
        oa = np.asarray(r["oa"], dtype=np.float32).reshape(3, SA, DA)
        ob = np.asarray(r["ob"], dtype=np.float32).reshape(3, SB, DA)
        cas[0, 2 * i:2 * i + 2] = oa[1:]
        cbs[0, 2 * i:2 * i + 2] = ob[1:]
        if i == NCORES - 1:
            cap = oa[0][None, None].copy()
            cbp = ob[0][None, None].copy()
    return (yp, ys, cap, cbp, cas, cbs)


def pick_group(FC):
    for g in (11, 8, 4, 2, 1):
        if FC % g == 0:
            return g
    return 1


def kernel(**inputs):
    D = inputs["x_prompt"].shape[-1]
    FF = inputs["ffn1_w_gate"].shape[-1]
    G = pick_group(FF // P)
    maps, dims = prep_inputs(inputs, G)
    nc = build_program(D, FF, G)
    res = run_bass_kernel_spmd(nc, maps, core_ids=list(range(NCORES)))
    return assemble(res.results, dims)
```
